# Optimizing a Trainium2 kernel written in Bass

```python
import jax, jax.numpy as jnp
from jax import lax
import numpy as np

D_MODEL = 4096
BATCH = 2
SEQ = 4096
DEPTH = 2

HEAD_DIM = 128
BRANCH_WIDTH = D_MODEL // 2
N_HEADS_A = BRANCH_WIDTH // HEAD_DIM
N_HEADS_C = BRANCH_WIDTH // HEAD_DIM
MOBA_BLOCK = 256
MOBA_TOPK = 3
Q_BLOCK = 128
ROPE_THETA = 500000.0
ROPE_DIM = HEAD_DIM // 4
CONV_WIDTH = 3
N_BRANCHES = 3
RMS_EPS = 1e-6
IN_COLS = 12 * BRANCH_WIDTH + N_BRANCHES * D_MODEL

kernel_name = 'hybrid_moba_shortconv_stickbreaking'


def rms_norm(x, g):
    xf = x.astype(jnp.float32)
    y = xf * lax.rsqrt(jnp.mean(xf * xf, axis=-1, keepdims=True) + RMS_EPS)
    return (y * g.astype(jnp.float32)).astype(x.dtype)


def split_heads(t, n_heads):
    b, s, _ = t.shape
    return t.reshape(b, s, n_heads, HEAD_DIM).transpose(0, 2, 1, 3)


def merge_heads(t):
    b, h, s, d = t.shape
    return t.transpose(0, 2, 1, 3).reshape(b, s, h * d)


def partial_rope(x, pos):
    half = ROPE_DIM // 2
    inv_freq = ROPE_THETA ** (-jnp.arange(half, dtype=jnp.float32) / half)
    ang = pos.astype(jnp.float32)[:, None] * inv_freq[None, :]
    cos = jnp.cos(ang).astype(x.dtype)
    sin = jnp.sin(ang).astype(x.dtype)
    x1 = x[..., :half]
    x2 = x[..., half:ROPE_DIM]
    rest = x[..., ROPE_DIM:]
    return jnp.concatenate([x1 * cos - x2 * sin, x2 * cos + x1 * sin, rest], axis=-1)


def moba_attention(q, k, v):
    b, h, s, d = q.shape
    n_blk = -(-s // MOBA_BLOCK)
    s_pad = n_blk * MOBA_BLOCK
    pad = ((0, 0), (0, 0), (0, s_pad - s), (0, 0))
    q = jnp.pad(q, pad)
    k = jnp.pad(k, pad)
    v = jnp.pad(v, pad)
    scale = d ** -0.5
    qf = q.reshape(b * h, s_pad, d)
    kb = k.reshape(b * h, n_blk, MOBA_BLOCK, d)
    vb = v.reshape(b * h, n_blk, MOBA_BLOCK, d)
    k_mean = jnp.mean(kb.astype(jnp.float32), axis=2)
    gate = jnp.einsum('nsd,nbd->nsb', qf.astype(jnp.float32), k_mean)
    q_blk = jnp.arange(s_pad) // MOBA_BLOCK
    past = jnp.arange(n_blk)[None, :] < q_blk[:, None]
    gate = jnp.where(past[None], gate, -jnp.inf)
    n_sel = min(MOBA_TOPK, n_blk)
    _, sel = lax.top_k(gate, n_sel)
    sel_ok = sel < q_blk[None, :, None]
    n_chunks = s_pad // Q_BLOCK
    bh_ids = jnp.repeat(jnp.arange(b * h), n_chunks)
    c_ids = jnp.tile(jnp.arange(n_chunks), b * h)

    def one_chunk(ids):
        bh, c = ids
        start = c * Q_BLOCK
        qc = lax.dynamic_slice_in_dim(qf[bh], start, Q_BLOCK, 0)
        idx = lax.dynamic_slice_in_dim(sel[bh], start, Q_BLOCK, 0)
        ok = lax.dynamic_slice_in_dim(sel_ok[bh], start, Q_BLOCK, 0)
        k_sel = kb[bh, idx]
        v_sel = vb[bh, idx]
        own = start // MOBA_BLOCK
        k_own = kb[bh, own]
        v_own = vb[bh, own]
        s_sel = jnp.einsum('qd,qnkd->qnk', qc, k_sel).astype(jnp.float32) * scale
        s_sel = jnp.where(ok[:, :, None], s_sel, -jnp.inf).reshape(Q_BLOCK, n_sel * MOBA_BLOCK)
        q_pos = start + jnp.arange(Q_BLOCK)
        k_pos = own * MOBA_BLOCK + jnp.arange(MOBA_BLOCK)
        s_own = jnp.einsum('qd,kd->qk', qc, k_own).astype(jnp.float32) * scale
        s_own = jnp.where(k_pos[None, :] <= q_pos[:, None], s_own, -jnp.inf)
        p = jax.nn.softmax(jnp.concatenate([s_sel, s_own], axis=-1), axis=-1).astype(v.dtype)
        p_sel = p[:, :n_sel * MOBA_BLOCK].reshape(Q_BLOCK, n_sel, MOBA_BLOCK)
        p_own = p[:, n_sel * MOBA_BLOCK:]
        return jnp.einsum('qnk,qnkd->qd', p_sel, v_sel) + jnp.einsum('qk,kd->qd', p_own, v_own)

    out = lax.map(one_chunk, (bh_ids, c_ids))
    return out.reshape(b, h, s_pad, d)[:, :, :s]


def stick_breaking_attention(q, k, v):
    b, h, s, d = q.shape
    n_blocks = s // Q_BLOCK
    scale = d ** -0.5
    k_pos = jnp.arange(s)
    qb = q.reshape(b, h, n_blocks, Q_BLOCK, d).transpose(2, 0, 1, 3, 4)

    def one_block(args):
        q_blk, i = args
        z = jnp.einsum('bhqd,bhkd->bhqk', q_blk, k).astype(jnp.float32) * scale
        q_pos = i * Q_BLOCK + jnp.arange(Q_BLOCK)
        causal = k_pos[None, :] < q_pos[:, None]
        log_keep = jnp.where(causal, jax.nn.log_sigmoid(-z), 0.0)
        later = lax.cumsum(log_keep, axis=log_keep.ndim - 1, reverse=True) - log_keep
        a = jnp.where(causal, jnp.exp(jax.nn.log_sigmoid(z) + later), 0.0).astype(v.dtype)
        return jnp.einsum('bhqk,bhkd->bhqd', a, v)

    out = lax.map(one_block, (qb, jnp.arange(n_blocks)))
    return out.transpose(1, 2, 0, 3, 4).reshape(b, h, s, d)


def short_conv_mixer(b_gate, c_gate, u, conv_w):
    xc = c_gate * u
    w = xc.shape[-1]
    y = lax.conv_general_dilated(
        xc, conv_w[:, None, :].astype(xc.dtype), window_strides=(1,),
        padding=[(CONV_WIDTH - 1, 0)], dimension_numbers=('NWC', 'WIO', 'NWC'),
        feature_group_count=w)
    return b_gate * y


def hybrid_layer(x, pre_g, post_g, w_in, b_merge, conv_w, p_a, p_b, p_c, w_o):
    b, s, _ = x.shape
    h = rms_norm(x, pre_g)
    proj = jnp.einsum('bsd,dc->bsc', h, w_in)
    cuts = [BRANCH_WIDTH * i for i in range(1, 13)]
    (aq, ak, av, az, bb, bc, bx, bz, cq, ck, cv, cz, g) = jnp.split(proj, cuts, axis=-1)
    pos = jnp.arange(s)
    qa = partial_rope(split_heads(aq, N_HEADS_A), pos)
    ka = partial_rope(split_heads(ak, N_HEADS_A), pos)
    o_a = merge_heads(moba_attention(qa, ka, split_heads(av, N_HEADS_A))) * jax.nn.silu(az)
    o_b = short_conv_mixer(bb, bc, bx, conv_w) * jax.nn.silu(bz)
    o_c = merge_heads(stick_breaking_attention(
        split_heads(cq, N_HEADS_C), split_heads(ck, N_HEADS_C), split_heads(cv, N_HEADS_C))) * jax.nn.silu(cz)
    gates = jax.nn.sigmoid((g + b_merge).reshape(b, s, N_BRANCHES, D_MODEL))
    y = (gates[:, :, 0] * jnp.einsum('bsw,wd->bsd', o_a, p_a)
         + gates[:, :, 1] * jnp.einsum('bsw,wd->bsd', o_b, p_b)
         + gates[:, :, 2] * jnp.einsum('bsw,wd->bsd', o_c, p_c))
    out = jnp.einsum('bsd,de->bse', y, w_o)
    return x + rms_norm(out, post_g)


def setup_inputs(seed: int = 0) -> dict:
    key = jax.random.key(seed)
    ks = jax.random.split(key, 10)
    f32 = jnp.float32
    x = jax.random.normal(ks[0], (BATCH, SEQ, D_MODEL), f32)
    pre_norm_gain = 1.0 + 0.05 * jax.random.normal(ks[1], (DEPTH, D_MODEL), f32)
    post_norm_gain = 1.0 + 0.05 * jax.random.normal(ks[2], (DEPTH, D_MODEL), f32)
    w_in = jax.random.normal(ks[3], (DEPTH, D_MODEL, IN_COLS), f32) * D_MODEL ** -0.5
    b_merge_gate = 0.01 * jax.random.normal(ks[4], (DEPTH, N_BRANCHES * D_MODEL), f32)
    conv_w = jax.random.normal(ks[5], (DEPTH, CONV_WIDTH, BRANCH_WIDTH), f32) * CONV_WIDTH ** -0.5
    w_branch_a = jax.random.normal(ks[6], (DEPTH, BRANCH_WIDTH, D_MODEL), f32) * BRANCH_WIDTH ** -0.5
    w_branch_b = jax.random.normal(ks[7], (DEPTH, BRANCH_WIDTH, D_MODEL), f32) * BRANCH_WIDTH ** -0.5
    w_branch_c = jax.random.normal(ks[8], (DEPTH, BRANCH_WIDTH, D_MODEL), f32) * BRANCH_WIDTH ** -0.5
    w_out = jax.random.normal(ks[9], (DEPTH, D_MODEL, D_MODEL), f32) * D_MODEL ** -0.5
    return {'x': x, 'pre_norm_gain': pre_norm_gain, 'post_norm_gain': post_norm_gain,
            'w_in': w_in, 'b_merge_gate': b_merge_gate, 'conv_w': conv_w,
            'w_branch_a': w_branch_a, 'w_branch_b': w_branch_b, 'w_branch_c': w_branch_c,
            'w_out': w_out}


def reference(x, pre_norm_gain, post_norm_gain, w_in, b_merge_gate, conv_w,
              w_branch_a, w_branch_b, w_branch_c, w_out):
    for layer in range(DEPTH):
        x = hybrid_layer(x, pre_norm_gain[layer], post_norm_gain[layer], w_in[layer],
                         b_merge_gate[layer], conv_w[layer], w_branch_a[layer],
                         w_branch_b[layer], w_branch_c[layer], w_out[layer])
    return x
```

```python
import numpy as np
import ml_dtypes
from contextlib import ExitStack
import concourse.bass as bass
import concourse.mybir as mybir
from concourse.bass_utils import run_bass_kernel_spmd

F32 = mybir.dt.float32
BF16 = mybir.dt.bfloat16
AF = mybir.ActivationFunctionType
ALU = mybir.AluOpType
AX = mybir.AxisListType

NCORES = 8
W = 2048
HD = 128
EPS = 1e-6
NEG = -30000.0
ROPE_THETA = 500000.0


class Cfg:
    def __init__(self, D=4096, S=4096, NB=2, DEPTH=2):
        self.D, self.S, self.NB, self.DEPTH = D, S, NB, DEPTH
        self.NT = NB * S
        self.DC = D // NCORES
        self.KCO = self.DC // 128
        self.KD = D // 128
        self.NTT = self.NT // 512
        self.TPS = S // 512
        self.NSUB = S // 128
        self.NBLK = S // 256
        self.NCOLS = 3072 + 3 * self.DC
        self.KQ = 4 if self.KD % 4 == 0 and self.KD >= 8 else 1
        self.KPQ = self.KD // self.KQ


class Sem:
    def __init__(self, nc, es, name):
        self.h = es.enter_context(nc.semaphore(name))
        self.n = 0


class Rec:
    ENG = ("sp", "act", "dve", "pool", "pe")

    def __init__(self):
        self.q = {k: [] for k in self.ENG}
        self.seen = {k: {} for k in self.ENG}

    def wait(self, eng, sem, val):
        if val <= 0:
            return
        if self.seen[eng].get(id(sem), 0) >= val:
            return
        self.seen[eng][id(sem)] = val
        self.q[eng].append(lambda e, h=sem.h, v=val: e.wait_ge(h, v))

    def op(self, eng, fn, inc=None, waits=(), k=None):
        agg = {}
        for (s, v) in waits:
            if id(s) not in agg or agg[id(s)][1] < v:
                agg[id(s)] = (s, v)
        for (s, v) in agg.values():
            self.wait(eng, s, v)
        if inc is not None:
            if k is None:
                k = 1
            inc.n += k
            self.q[eng].append(lambda e, fn=fn, h=inc.h, k=k: fn(e).then_inc(h, k))
            return inc.n
        self.q[eng].append(fn)
        return None

    def dma(self, eng, out, in_, sem, waits=()):
        return self.op(eng, lambda e, o=out, i=in_: e.dma_start(out=o, in_=i), inc=sem, waits=waits, k=16)

    def flush(self, nc):
        q = self.q
        with nc.Block() as block:
            if q["sp"]:
                @block.sync
                def _(e):
                    for f in q["sp"]:
                        f(e)
            if q["act"]:
                @block.scalar
                def _(e):
                    for f in q["act"]:
                        f(e)
            if q["dve"]:
                @block.vector
                def _(e):
                    for f in q["dve"]:
                        f(e)
            if q["pool"]:
                @block.gpsimd
                def _(e):
                    for f in q["pool"]:
                        f(e)
            if q["pe"]:
                @block.tensor
                def _(e):
                    for f in q["pe"]:
                        f(e)
        self.q = {k: [] for k in self.ENG}


def host_consts(cfg):
    S = cfg.S
    bf = ml_dtypes.bfloat16
    c = {}
    p = np.arange(128)[:, None]
    f = np.arange(512)[None, :]
    cm = np.zeros((8, 128, 512), np.float32)
    for i in range(4):
        kpos = i * 128 + p
        cm[i] = np.where(kpos <= f, 0.0, NEG)
        cm[4 + i] = np.where(kpos < f, 0.0, NEG)
    c["cm"] = cm.astype(bf)
    half = 16
    inv_freq = (ROPE_THETA ** (-np.arange(half, dtype=np.float32) / half)).astype(np.float32)
    ang = np.arange(S, dtype=np.float32)[None, :] * inv_freq[:, None]
    cos = np.cos(ang).astype(np.float32)
    sin = np.sin(ang).astype(np.float32)
    c["ropec"] = np.concatenate([cos, cos], 0).astype(np.float32)
    c["ropes"] = np.concatenate([-sin, sin], 0).astype(np.float32)
    sub = np.arange(cfg.NSUB)[:, None]
    b = np.arange(16)[None, :]
    own = sub // 2
    pastb = np.where(b < own, 0.0, -1e30).astype(np.float32)
    ownf = (b == own).astype(np.float32)
    c["pastb"] = np.broadcast_to(pastb.reshape(1, -1), (128, cfg.NSUB * 16)).copy()
    c["ownf"] = np.broadcast_to(ownf.reshape(1, -1), (128, cfg.NSUB * 16)).copy()
    c["ident"] = np.eye(128, dtype=np.float32).astype(bf)
    c["ones"] = np.ones((128, 128), np.float32).astype(bf)
    c["onesf"] = np.ones((1, 128), np.float32)
    j = np.arange(128)[:, None]
    k = np.arange(128)[None, :]
    c["tincl"] = np.where(j >= k, -1.0, 0.0).astype(bf)
    u2 = np.zeros((64, 32, 128), np.float32)
    for jc in range(32):
        for kc in range(32):
            if jc > kc:
                u2[jc, kc, :] = -1.0
                u2[32 + jc, kc, :] = -1.0
    c["u2"] = u2.astype(bf)
    z95 = np.zeros((128, 95), np.float32)
    z95[:, 31] = 1.0
    z95[:, 63] = 1.0
    c["z95"] = z95.astype(bf)
    ind = np.zeros((16, 16, 128), np.float32)
    for bb in range(16):
        ind[bb, bb, :] = -NEG
    c["indbig"] = ind.astype(bf)
    psw = np.zeros((128, 32), np.float32)
    for m in range(16):
        psw[16 + m, m] = 1.0
        psw[m, 16 + m] = 1.0
    c["psw"] = psw.astype(bf)
    return c


CONST_SPECS = None


def const_specs(cfg):
    return {
        "cm": ([8, 128, 512], BF16), "ropec": ([32, cfg.S], F32), "ropes": ([32, cfg.S], F32),
        "pastb": ([128, cfg.NSUB * 16], F32), "ownf": ([128, cfg.NSUB * 16], F32),
        "ident": ([128, 128], BF16), "ones": ([128, 128], BF16), "onesf": ([1, 128], F32),
        "tincl": ([128, 128], BF16), "u2": ([64, 32, 128], BF16), "z95": ([128, 95], BF16),
        "indbig": ([16, 16, 128], BF16), "psw": ([128, 32], BF16),
    }


def build(cfg, seg):
    D, S, NB, NT, DC, KCO, KD, NTT, TPS = cfg.D, cfg.S, cfg.NB, cfg.NT, cfg.DC, cfg.KCO, cfg.KD, cfg.NTT, cfg.TPS
    NSUB, NCOLS, KQ, KPQ, NBLK = cfg.NSUB, cfg.NCOLS, cfg.KQ, cfg.KPQ, cfg.NBLK
    SCALE = HD ** -0.5
    NWH = 4 if KD >= 16 else 1
    KH = KD // NWH
    nc = bass.Bass("TRN2", target_bir_lowering=False)
    dt = nc.dram_tensor
    import os as _os
    CHAIN = int(_os.environ.get("CHAIN", "1"))

    def IN(name, shape, d):
        return dt(name, shape, d, kind="ExternalInput").ap()

    def OUT(name, shape, d):
        return dt(name, shape, d, kind="ExternalOutput").ap()
    l = 0
    cd = {}
    if seg == "S0":
        xT = IN("xT", [DC, NT], F32)
        ssp = OUT("ssp", [NTT, 512], F32)
    if seg == "S1":
        xT = IN("xT", [DC, NT], F32)
        ssall = IN("ssall", [8, NTT, 512], F32)
        pre_g = IN("pre_g", [128, KCO], F32)
        hin = OUT("hin", [DC, NT], BF16)
    if seg == "S2":
        hfull = IN("hfull", [D, NT], BF16)
        w_in = IN("w_in", [1, D, NCOLS], F32)
        b_mg = IN("b_mg", [128, 3 * KCO], F32)
        conv_w = IN("conv_w", [128, 6], F32)
        cs_ = const_specs(cfg)
        cd = {k: IN(k, shp, d) for k, (shp, d) in cs_.items()}
        oin = OUT("oin", [768, NT], BF16)
        sig = OUT("sig", [3 * DC, NT], BF16)
    if seg == "S3":
        ofull = IN("ofull", [8 * 768, NT], BF16)
        sig = IN("sig", [3 * DC, NT], BF16)
        p_w = IN("p_w", [3, W, DC], F32)
        yin = OUT("yin", [DC, NT], BF16)
    if seg == "S4":
        yfull = IN("yfull", [D, NT], BF16)
        w_o = IN("w_o", [D, DC], F32)
        oscr = OUT("oscr", [DC, NT], F32)
        ssp = OUT("ssp", [NTT, 512], F32)
    if seg == "S5":
        oscr = IN("oscr", [DC, NT], F32)
        xT = IN("xT", [DC, NT], F32)
        ssall = IN("ssall", [8, NTT, 512], F32)
        post_g = IN("post_g", [128, KCO], F32)
        xnT = OUT("xnT", [DC, NT], F32)
        ssp = OUT("ssp", [NTT, 512], F32)
    uid = [0]

    es = ExitStack()
    with es:
        def SBs(scope, name, shape, d):
            uid[0] += 1
            return scope.enter_context(nc.sbuf_tensor(f"sb{uid[0]}_{name}", shape, d))
        sems = {}

        def SM(name):
            if name not in sems:
                sems[name] = Sem(nc, es, name)
            return sems[name]

        R = Rec()
        if _os.environ.get("EARLYSEM"):
            SM("fs_ld"); SM("fs_st")
        SE = {k: SM("ev_" + k) for k in ("pe", "act", "dve", "pool")}

        def E(eng, fn, waits=(), chain=True):
            w = list(waits)
            if CHAIN and chain and eng != "pe" and SE[eng].n > 0:
                w.append((SE[eng], SE[eng].n))
            return (SE[eng], R.op(eng, fn, inc=SE[eng], waits=w))

        def N(eng, fn, waits=()):
            if eng != "pe":
                E(eng, fn, waits)
                return
            R.op(eng, fn, waits=waits)

        class Ring:
            def __init__(self, scope, name, n, shape, d):
                self.t = [SBs(scope, f"{name}{i}", shape, d) for i in range(n)]
                self.ld = [SM(f"{name}_ld{i}") for i in range(n)]
                self.st = [SM(f"{name}_st{i}") for i in range(n)]
                self.n = n

            def load(self, i, out_ap, in_ap, waits=(), eng="sp"):
                b = i % self.n
                return (self.ld[b], R.dma(eng, out_ap, in_ap, self.ld[b], waits=waits))

            def store(self, i, out_ap, in_ap, waits=(), eng="pool"):
                b = i % self.n
                return (self.st[b], R.dma(eng, out_ap, in_ap, self.st[b], waits=waits))

            def st_done(self, i):
                b = i % self.n
                return (self.st[b], self.st[b].n)

        ones = SBs(es, "ones", [128, 128], BF16)
        ones8 = SBs(es, "ones8", [8, 128], F32)
        banks = [es.enter_context(nc.psum_tensor(f"bank{i}", [128, 512], F32)) for i in range(8)]
        N("pool", lambda e: e.memset(ones[:], 1.0))
        N("pool", lambda e: e.memset(ones8[:], 1.0))
        s_c = SM("const")
        if seg == "S2":
            ident = SBs(es, "ident", [128, 128], BF16)
            cm = SBs(es, "cm", [128, 8, 512], BF16)
            tincl = SBs(es, "tincl", [128, 128], BF16)
            u2 = SBs(es, "u2", [64, 32, 128], BF16)
            z95 = SBs(es, "z95", [128, 95], BF16)
            indbig = SBs(es, "indbig", [16, 16, 128], BF16)
            psw = SBs(es, "psw", [128, 32], BF16)
            pastb = SBs(es, "pastb", [128, NSUB * 16], F32)
            ownf = SBs(es, "ownf", [128, NSUB * 16], F32)
            bmg = SBs(es, "bmg", [128, 3 * KCO], F32)
            cw = SBs(es, "cw", [128, 6], F32)
            for name, t in (("ident", ident), ("tincl", tincl), ("u2", u2),
                            ("z95", z95), ("indbig", indbig), ("psw", psw), ("pastb", pastb), ("ownf", ownf)):
                R.dma("sp", t[:], cd[name], s_c)
            R.dma("sp", cm[:], cd["cm"].rearrange("i p f -> p i f"), s_c)
            R.dma("sp", bmg[:], b_mg, s_c)
            R.dma("sp", cw[:], conv_w, s_c)
        if seg == "S1":
            gpre = SBs(es, "gpre", [128, KCO], F32)
            R.dma("sp", gpre[:], pre_g, s_c)
        if seg == "S5":
            gpost = SBs(es, "gpost", [128, KCO], F32)
            R.dma("sp", gpost[:], post_g, s_c)
        for eng in ("sp", "act", "dve", "pool", "pe"):
            R.wait(eng, s_c, s_c.n)
            R.wait(eng, SE["pool"], SE["pool"].n)

        def section_end():
            R.flush(nc)

        def sumsq_tile(tt, srcs_bf, sst_ring, cnt, extra_waits, bk):
            w = list(extra_waits)
            last = None
            for i, a in enumerate(srcs_bf):
                fn = (lambda e, a=a, i=i: e.matmul(bk[0:1, :], lhsT=ones[:, 0:1], rhs=a, start=(i == 0),
                                                   stop=(i == len(srcs_bf) - 1)))
                if i == len(srcs_bf) - 1:
                    last = E("pe", fn, waits=w if i == 0 else ())
                else:
                    N("pe", fn, waits=w if i == 0 else ())
            b = cnt % sst_ring.n
            cp = E("dve", lambda e, b=b: e.tensor_copy(out=sst_ring.t[b][0:1, :], in_=bk[0:1, :]),
                   waits=[last, sst_ring.st_done(cnt)])
            sst_ring.store(cnt, ssp[tt:tt + 1, :], sst_ring.t[b][0:1, :], waits=[cp])
            return cp

        def phase0(src):
            with ExitStack() as ps:
                xt = Ring(ps, "p0x", 2, [128, KCO, 512], F32)
                sq = Ring(ps, "p0s", 2, [128, KCO, 512], BF16)
                sst = Ring(ps, "p0t", 2, [1, 512], F32)
                sq_ev = [None] * NTT
                cp_ev = [None] * NTT
                for tt in range(NTT):
                    b = tt % 2
                    ld = xt.load(tt, xt.t[b][:], src[:, tt * 512:(tt + 1) * 512].rearrange("(k p) t -> p k t", p=128),
                                 waits=[sq_ev[tt - 2]] if tt >= 2 else [])
                    w = [ld] + ([cp_ev[tt - 2]] if tt >= 2 else [])
                    sq_ev[tt] = E("act", lambda e, b=b: e.activation(out=sq.t[b][:], in_=xt.t[b][:], func=AF.Square),
                                  waits=w)
                    w2 = [sq_ev[tt]] + ([cp_ev[tt - 1]] if tt >= 1 else [])
                    cp_ev[tt] = sumsq_tile(tt, [sq.t[b][:, kc, :] for kc in range(KCO)], sst, tt, w2, banks[7])
                fin = [sst.st_done(0), sst.st_done(1)]
                for ev in fin:
                    R.wait("pool", *ev)
                section_end()

        def rstd_bcast(tt, ss8, rb, prev_waits):
            b = tt % 2
            ld = ss8.load(tt, ss8.t[b][:], ssall[:, tt, :], waits=prev_waits)
            bk = banks[tt % 2]
            pe = E("pe", lambda e, b=b, bk=bk: e.matmul(bk[:, :], lhsT=ones8[:, :], rhs=ss8.t[b][:], start=True, stop=True),
                   waits=[ld] + list(prev_waits))
            a = E("act", lambda e, b=b, bk=bk: e.activation(out=rb[b][:, :], in_=bk[:, :], func=AF.Sqrt, bias=EPS, scale=1.0 / D),
                  waits=[pe])
            return E("dve", lambda e, b=b: e.reciprocal(out=rb[b][:, :], in_=rb[b][:, :]), waits=[a])

        def phase1(src):
            with ExitStack() as ps:
                xt = Ring(ps, "p1x", 2, [128, KCO, 512], F32)
                ss8 = Ring(ps, "p1s", 2, [8, 512], F32)
                rb = [SBs(ps, f"p1r{i}", [128, 512], F32) for i in range(2)]
                hs = Ring(ps, "p1h", 2, [128, KCO, 512], BF16)
                dv = [None] * NTT
                for tt in range(NTT):
                    b = tt % 2
                    pw = [dv[tt - 2]] if tt >= 2 else []
                    l1 = xt.load(tt, xt.t[b][:], src[:, tt * 512:(tt + 1) * 512].rearrange("(k p) t -> p k t", p=128), waits=pw)
                    r = rstd_bcast(tt, ss8, rb, pw)
                    w = [l1, r, hs.st_done(tt)]
                    for kc in range(KCO):
                        dv[tt] = E("dve", lambda e, b=b, kc=kc: e.scalar_tensor_tensor(
                            out=hs.t[b][:, kc, :], in0=xt.t[b][:, kc, :], scalar=gpre[:, kc:kc + 1], in1=rb[b][:, :],
                            op0=ALU.mult, op1=ALU.mult), waits=w if kc == 0 else ())
                    hs.store(tt, hin[:, tt * 512:(tt + 1) * 512].rearrange("(k p) t -> p k t", p=128), hs.t[b][:], waits=[dv[tt]])
                for i in range(2):
                    R.wait("pool", *hs.st_done(i))
                section_end()

        def phase5(src):
            with ExitStack() as ps:
                xt = Ring(ps, "p5x", 2, [128, KCO, 512], F32)
                ot = Ring(ps, "p5o", 2, [128, KCO, 512], F32)
                ss8 = Ring(ps, "p5s", 2, [8, 512], F32)
                rb = [SBs(ps, f"p5r{i}", [128, 512], F32) for i in range(2)]
                xs = Ring(ps, "p5y", 2, [128, KCO, 512], F32)
                sq = [SBs(ps, f"p5q{i}", [128, KCO, 512], BF16) for i in range(2)]
                sst = Ring(ps, "p5t", 2, [1, 512], F32)
                dv = [None] * NTT
                sqe = [None] * NTT
                cpe = [None] * NTT
                for tt in range(NTT):
                    b = tt % 2
                    pw = [dv[tt - 2]] if tt >= 2 else []
                    l1 = xt.load(tt, xt.t[b][:], src[:, tt * 512:(tt + 1) * 512].rearrange("(k p) t -> p k t", p=128), waits=pw)
                    l2 = ot.load(tt, ot.t[b][:], oscr[:, tt * 512:(tt + 1) * 512].rearrange("(k p) t -> p k t", p=128), waits=pw)
                    r = rstd_bcast(tt, ss8, rb, pw)
                    w = [l1, l2, r, xs.st_done(tt)] + ([sqe[tt - 2]] if tt >= 2 else [])
                    for kc in range(KCO):
                        E("dve", lambda e, b=b, kc=kc: e.scalar_tensor_tensor(
                            out=ot.t[b][:, kc, :], in0=ot.t[b][:, kc, :], scalar=gpost[:, kc:kc + 1], in1=rb[b][:, :],
                            op0=ALU.mult, op1=ALU.mult), waits=w if kc == 0 else ())
                        dv[tt] = E("dve", lambda e, b=b, kc=kc: e.tensor_tensor(
                            out=xs.t[b][:, kc, :], in0=ot.t[b][:, kc, :], in1=xt.t[b][:, kc, :], op=ALU.add))
                    xs.store(tt, xnT[:, tt * 512:(tt + 1) * 512].rearrange("(k p) t -> p k t", p=128), xs.t[b][:], waits=[dv[tt]])
                    sqe[tt] = E("act", lambda e, b=b: e.activation(out=sq[b][:], in_=xs.t[b][:], func=AF.Square),
                                waits=[dv[tt]] + ([cpe[tt - 2]] if tt >= 2 else []))
                    cpe[tt] = sumsq_tile(tt, [sq[b][:, kc, :] for kc in range(KCO)], sst, tt,
                                         [sqe[tt]] + ([cpe[tt - 1]] if tt >= 1 else []), banks[7])
                for i in range(2):
                    R.wait("pool", *xs.st_done(i))
                    R.wait("pool", *sst.st_done(i))
                section_end()

        class ProjEngine:
            def __init__(self, common):
                self.state = {"wcnt": 0, "hcnt": 0, "tile": 0}
                self.wst = Ring(common, "wst", 2, [128, KH, 128], F32)
                self.wu = SBs(common, "wu", [128, KD, 512], BF16)
                self.hq = Ring(common, "hq", 3, [128, KPQ, 512], BF16)
                self.h_end = {}
                self.tile_ev = {}
                self.wu_free = [None]
                self.cast_prev = {}

            def load_weights(self, w2d, col0, ncb):
                state, wst, wu = self.state, self.wst, self.wu
                cast_ev = None
                for cb in range(ncb):
                    for hf in range(NWH):
                        i = state["wcnt"]
                        state["wcnt"] += 1
                        b = i % 2
                        src = w2d[:, col0 + cb * 128: col0 + (cb + 1) * 128].rearrange("(k p) c -> p k c", p=128)[:, hf * KH:(hf + 1) * KH, :]
                        prev = self.cast_prev
                        ld = wst.load(i, wst.t[b][:], src, waits=[prev[i - 2]] if (i - 2) in prev else [])
                        w = [ld] + ([self.wu_free[0]] if self.wu_free[0] is not None else [])
                        cast_ev = E("pool", lambda e, b=b, cb=cb, hf=hf: e.tensor_copy(
                            out=wu[:, hf * KH:(hf + 1) * KH, cb * 128:(cb + 1) * 128], in_=wst.t[b][:]), waits=w)
                        prev[i] = cast_ev
                return cast_ev

            def project(self, srcT, ncb, tts, evac_tile, w_ready, pre_waits):
                state, wu, hq, h_end, tile_ev = self.state, self.wu, self.hq, self.h_end, self.tile_ev
                R.wait("pe", *w_ready)
                for (s, v) in pre_waits:
                    R.wait("pe", s, v)
                for ti, tt in enumerate(tts):
                    tl = state["tile"]
                    state["tile"] += 1
                    bs = [banks[(tl % 2) * 4 + cb] for cb in range(ncb)]
                    if (tl - 2) in tile_ev:
                        for ev in tile_ev[tl - 2]:
                            R.wait("pe", *ev)
                    stops = [None] * ncb
                    for q in range(KQ):
                        hi = state["hcnt"]
                        state["hcnt"] += 1
                        hb = hi % hq.n
                        ld = hq.load(hi, hq.t[hb][:],
                                     srcT[:, tt * 512:(tt + 1) * 512].rearrange("(k p) t -> p k t", p=128)[:, q * KPQ:(q + 1) * KPQ, :],
                                     waits=[h_end[hi - hq.n]] if (hi - hq.n) in h_end else [])
                        R.wait("pe", *ld)
                        for cb in range(ncb):
                            for k in range(KPQ):
                                kk = q * KPQ + k
                                fn = (lambda e, cb=cb, kk=kk, k=k, hb=hb, bs=bs: e.matmul(
                                    bs[cb][:, :], lhsT=wu[:, kk, cb * 128:(cb + 1) * 128], rhs=hq.t[hb][:, k, :],
                                    start=(kk == 0), stop=(kk == KD - 1)))
                                is_stop = (kk == KD - 1)
                                is_qend = (cb == ncb - 1 and k == KPQ - 1)
                                if is_stop or is_qend:
                                    ev = E("pe", fn)
                                    if is_stop:
                                        stops[cb] = ev
                                    if is_qend:
                                        h_end[hi] = ev
                                        self.wu_free[0] = ev
                                else:
                                    N("pe", fn)
                    tile_ev[tl] = evac_tile(ti, tt, bs, stops)

            def all_tile_events(self):
                evs = []
                for tl in (self.state["tile"] - 1, self.state["tile"] - 2):
                    if tl in self.tile_ev:
                        evs += self.tile_ev[tl]
                return evs

        def phase2():
            common = ExitStack()
            PJ = ProjEngine(common)
            state = PJ.state
            state["ocnt"] = 0
            ost = Ring(common, "ost", 4, [128, 512], BF16)
            w2d = w_in[0]

            def load_weights(col0, ncb):
                return PJ.load_weights(w2d, col0, ncb)

            def project(ncb, tts, evac_tile, w_ready, pre_waits):
                return PJ.project(hfull, ncb, tts, evac_tile, w_ready, pre_waits)

            def all_tile_events():
                return PJ.all_tile_events()

            def store_o(src_ap_fn, row0, tok0, waits_fn):
                i = state["ocnt"]
                state["ocnt"] += 1
                b = i % ost.n
                ev = waits_fn(ost.t[b], ost.st_done(i))
                ost.store(i, oin[row0:row0 + 128, tok0:tok0 + 512], ost.t[b][:], waits=[ev])

            unit_done = []

            def attention_units(kind):
                with ExitStack() as ps:
                    qT = SBs(ps, "qT", [128, S], BF16)
                    kT = SBs(ps, "kT", [128, S], BF16)
                    vT = SBs(ps, "vT", [128, S], BF16)
                    zT = SBs(ps, "zT", [128, S], BF16)
                    V = SBs(ps, "V", [128, NSUB, 128], BF16)
                    Pt = [SBs(ps, f"Pt{i}", [128, 512], BF16) for i in range(3)]
                    otmp = SBs(ps, "otmp", [128, 512], F32)
                    if kind == "A":
                        selT = SBs(ps, "selT", [16, S], BF16)
                        gm = SBs(ps, "gm", [128, NSUB * 16], F32)
                        ga = SBs(ps, "ga", [128, NSUB * 16], F32)
                        gb = SBs(ps, "gb", [128, NSUB * 16], F32)
                        mr = SBs(ps, "mr", [128, NSUB], F32)
                        selm = SBs(ps, "selm", [128, NSUB * 16], BF16)
                        kmf = SBs(ps, "kmf", [128, 16], F32)
                        kmT = SBs(ps, "kmT", [128, 16], BF16)
                        rope = Ring(ps, "rope", 2, [32, 2, 512], F32)
                        rt1 = SBs(ps, "rt1", [32, 512], F32)
                        rt2 = SBs(ps, "rt2", [32, 512], F32)
                        rcp = SBs(ps, "rcp", [128, 512], F32)
                        N("dve", lambda e: e.memset(kmf[:], 0.0))
                    else:
                        SP = SBs(ps, "SP", [128, 4 * TPS, 512], BF16)
                        Et = [SBs(ps, f"Et{i}", [128, 512], F32) for i in range(2)]
                        CSs = [SBs(ps, f"CSs{i}", [64, 512], BF16) for i in range(2)]
                    base_col = 0 if kind == "A" else 1024
                    branch_row = 0 if kind == "A" else 512
                    for hh in range(2):
                        w_ready = load_weights(base_col + hh * 512, 4)
                        for b in range(NB):
                            prev_done = list(unit_done)
                            del unit_done[:]

                            def evac_tile(ti, tt, bs, stops, prev_done=prev_done):
                                s0 = ti * 512
                                pw = prev_done if ti == 0 else []
                                e0 = E("act", lambda e: e.activation(out=qT[:, s0:s0 + 512], in_=bs[0][:, :], func=AF.Copy, scale=SCALE),
                                       waits=[stops[0]] + pw)
                                e1 = E("dve", lambda e: e.tensor_copy(out=kT[:, s0:s0 + 512], in_=bs[1][:, :]), waits=[stops[1]] + pw)
                                e2 = E("dve", lambda e: e.tensor_copy(out=vT[:, s0:s0 + 512], in_=bs[2][:, :]), waits=[stops[2]])
                                e3 = E("act", lambda e: e.activation(out=zT[:, s0:s0 + 512], in_=bs[3][:, :], func=AF.Silu), waits=[stops[3]])
                                return [e0, e1, e2, e3]

                            project(4, [b * TPS + t for t in range(TPS)], evac_tile, w_ready, prev_done)
                            pj_done = all_tile_events()
                            for ev in pj_done:
                                for eng in ("pe", "act", "dve"):
                                    R.wait(eng, *ev)
                            row0 = branch_row + hh * 128
                            if kind == "A":
                                fin = moba(qT, kT, vT, zT, V, Pt, otmp, selT, gm, ga, gb, mr, selm, kmf, kmT, rope, rt1, rt2, rcp,
                                           row0, b * S)
                            else:
                                fin = sbatt(qT, kT, vT, zT, V, Pt, otmp, SP, Et, CSs, row0, b * S)
                            unit_done.extend(fin)
                    for ev in unit_done:
                        for eng in ("pe", "act", "dve", "pool", "sp"):
                            R.wait(eng, *ev)
                    del unit_done[:]
                    section_end()

            def v_transpose(vT, V):
                bk = banks[7].bitcast(BF16)
                last_cp = None
                for g in range(NSUB // 8):
                    w = [last_cp] if last_cp is not None else []
                    for j in range(8):
                        kc = g * 8 + j
                        fn = (lambda e, j=j, kc=kc: e.transpose(out=bk[:, j * 128:(j + 1) * 128], in_=vT[:, kc * 128:(kc + 1) * 128],
                                                                identity=ident[:, :]))
                        if j == 7:
                            pe = E("pe", fn, waits=w if j == 0 else ())
                        else:
                            N("pe", fn, waits=w if j == 0 else ())
                    last_cp = E("dve", lambda e, g=g: e.tensor_copy(out=V[:, g * 8:(g + 1) * 8, :].rearrange("p a d -> p (a d)"), in_=bk[:, :]),
                                waits=[pe])
                return last_cp

            def moba(qT, kT, vT, zT, V, Pt, otmp, selT, gm, ga, gb, mr, selm, kmf, kmT, rope, rt1, rt2, rcp, row0, tok0):
                bk7 = banks[7]
                last = None
                rcnt = moba.__dict__.setdefault("rcnt", 0)
                for t in range(TPS):
                    rb_ = rcnt % 2
                    w = [last] if last is not None else []
                    l1 = rope.load(rcnt, rope.t[rb_][:, 0, :], cd["ropec"][:, t * 512:(t + 1) * 512], waits=w)
                    l2 = rope.load(rcnt, rope.t[rb_][:, 1, :], cd["ropes"][:, t * 512:(t + 1) * 512])
                    rcnt += 1
                    for X in (qT, kT):
                        Xt = X[:, t * 512:(t + 1) * 512]
                        pe = E("pe", lambda e, Xt=Xt: e.matmul(bk7[0:32, :], lhsT=psw[:, :], rhs=Xt, start=True, stop=True),
                               waits=[last] if last is not None else [])
                        N("dve", lambda e, Xt=Xt, rb_=rb_: e.tensor_tensor(out=rt1[:, :], in0=Xt[0:32, :], in1=rope.t[rb_][:, 0, :], op=ALU.mult),
                          waits=[l1, l2, pe])
                        a = E("dve", lambda e, rb_=rb_: e.tensor_tensor(out=rt2[:, :], in0=bk7[0:32, :], in1=rope.t[rb_][:, 1, :], op=ALU.mult))
                        last = E("dve", lambda e, Xt=Xt: e.tensor_tensor(out=Xt[0:32, :], in0=rt1[:, :], in1=rt2[:, :], op=ALU.add), waits=[a])
                moba.__dict__["rcnt"] = rcnt
                R.wait("pe", *last)
                vdone = v_transpose(vT, V)
                N("dve", lambda e: e.tensor_reduce(out=kmf[:, 0:NBLK], in_=kT[:, :].rearrange("p (b t) -> p b t", t=256),
                                                   axis=AX.X, op=ALU.add), waits=[last])
                km = E("dve", lambda e: e.tensor_scalar(out=kmT[:, :], in0=kmf[:, :], scalar1=1.0 / 256.0, scalar2=None, op0=ALU.mult))
                for sub in range(NSUB):
                    fn = (lambda e, sub=sub: e.matmul(bk7[:, sub * 16:(sub + 1) * 16], lhsT=qT[:, sub * 128:(sub + 1) * 128],
                                                      rhs=kmT[:, :], start=True, stop=True))
                    if sub == NSUB - 1:
                        gpe = E("pe", fn)
                    else:
                        N("pe", fn, waits=[km, vdone] if sub == 0 else ())
                G = NSUB * 16
                N("dve", lambda e: e.tensor_tensor(out=gm[:, :], in0=bk7[:, 0:G], in1=pastb[:, :], op=ALU.add), waits=[gpe])
                cur = gm
                nxts = [ga, gb]
                for r in range(3):
                    a = E("dve", lambda e, cur=cur: e.tensor_reduce(out=mr[:, :], in_=cur[:, :].rearrange("p (s b) -> p s b", b=16),
                                                                    axis=AX.X, op=ALU.max))
                    if r < 2:
                        nx = nxts[r]
                        bcast = mr[:, :].unsqueeze(2).broadcast_to([128, NSUB, 16])
                        b_ = E("dve", lambda e, cur=cur, nx=nx, bcast=bcast: e.tensor_tensor(
                            out=nx[:, :].rearrange("p (s b) -> p s b", b=16), in0=cur[:, :].rearrange("p (s b) -> p s b", b=16),
                            in1=bcast, op=ALU.is_ge), waits=[a])
                        c_ = E("dve", lambda e, cur=cur, nx=nx: e.scalar_tensor_tensor(
                            out=nx[:, :], in0=nx[:, :], scalar=-1e30, in1=cur[:, :], op0=ALU.mult, op1=ALU.add), waits=[b_])
                        cur = nx
                        R.wait("dve", *c_)
                bcast = mr[:, :].unsqueeze(2).broadcast_to([128, NSUB, 16])
                s1 = E("dve", lambda e: e.tensor_tensor(out=ga[:, :].rearrange("p (s b) -> p s b", b=16),
                                                        in0=gm[:, :].rearrange("p (s b) -> p s b", b=16), in1=bcast, op=ALU.is_ge), waits=[a])
                s2 = E("dve", lambda e: e.tensor_tensor(out=ga[:, :], in0=ga[:, :], in1=ownf[:, :], op=ALU.max), waits=[s1])
                s3 = E("dve", lambda e: e.tensor_scalar(out=selm[:, :], in0=ga[:, :], scalar1=-1.0, scalar2=None, op0=ALU.add), waits=[s2])
                bk = banks[7].bitcast(BF16)
                last_cp = None
                ngrp = (NSUB + 7) // 8
                for g in range(ngrp):
                    w = [s3] + ([last_cp] if last_cp is not None else [])
                    nj = min(8, NSUB - g * 8)
                    for j in range(nj):
                        sub = g * 8 + j
                        fn = (lambda e, j=j, sub=sub: e.transpose(out=bk[0:16, j * 128:(j + 1) * 128], in_=selm[:, sub * 16:(sub + 1) * 16],
                                                                  identity=ident[:, :]))
                        if j == nj - 1:
                            pe = E("pe", fn, waits=w if j == 0 else ())
                        else:
                            N("pe", fn, waits=w if j == 0 else ())
                    last_cp = E("dve", lambda e, g=g, nj=nj: e.tensor_copy(out=selT[0:16, g * 1024:g * 1024 + nj * 128], in_=bk[0:16, 0:nj * 128]),
                                waits=[pe])
                R.wait("pe", *last_cp)
                slots = [(qt, kc) for qt in range(TPS) for kc in range(4 * qt + 4)]
                NS = len(slots)
                Wb = [banks[0], banks[1], banks[2]]
                OUTb = [banks[3], banks[4]]
                SUMb = [banks[5], banks[6]]
                w_ev = [None] * NS
                e_ev = [None] * NS
                pv_ev = [None] * NS
                epi_ev = {}

                def issue_W(i):
                    qt, kc = slots[i]
                    bk_ = Wb[i % 3]
                    w = [e_ev[i - 3]] if i >= 3 else []
                    diag = kc >= 4 * qt
                    N("pe", lambda e: e.matmul(bk_[:, :], lhsT=kT[:, kc * 128:(kc + 1) * 128], rhs=qT[:, qt * 512:(qt + 1) * 512],
                                               start=True, stop=False), waits=w)
                    fn2 = (lambda e: e.matmul(bk_[:, :], lhsT=indbig[:, kc // 2, :], rhs=selT[0:16, qt * 512:(qt + 1) * 512],
                                              start=False, stop=not diag))
                    if diag:
                        N("pe", fn2)
                        w_ev[i] = E("pe", lambda e: e.matmul(bk_[:, :], lhsT=ident[:, :], rhs=cm[:, kc - 4 * qt, :], start=False, stop=True))
                    else:
                        w_ev[i] = E("pe", fn2)

                def issue_exp(i):
                    bk_ = Wb[i % 3]
                    w = [w_ev[i]] + ([pv_ev[i - 3]] if i >= 3 else [])
                    e_ev[i] = E("act", lambda e: e.activation(out=Pt[i % 3][:, :], in_=bk_[:, :], func=AF.Exp), waits=w)

                def issue_PV(i):
                    qt, kc = slots[i]
                    n = 4 * qt + 4
                    w = [e_ev[i]]
                    if kc == 0 and (qt - 2) in epi_ev:
                        w.append(epi_ev[qt - 2])
                    N("pe", lambda e: e.matmul(OUTb[qt % 2][:, :], lhsT=V[:, kc, :], rhs=Pt[i % 3][:, :], start=(kc == 0), stop=(kc == n - 1)),
                      waits=w)
                    pv_ev[i] = E("pe", lambda e: e.matmul(SUMb[qt % 2][:, :], lhsT=ones[:, :], rhs=Pt[i % 3][:, :], start=(kc == 0),
                                                          stop=(kc == n - 1)))
                    if kc == n - 1:
                        t0 = qt * 512
                        a = E("dve", lambda e: e.reciprocal(out=rcp[:, :], in_=SUMb[qt % 2][:, :]), waits=[pv_ev[i]])
                        b_ = E("dve", lambda e: e.tensor_tensor(out=otmp[:, :], in0=OUTb[qt % 2][:, :], in1=rcp[:, :], op=ALU.mult), waits=[a])
                        epi_ev[qt] = b_

                        def wr(buf, stdone, t0=t0, b_=b_):
                            return E("dve", lambda e: e.tensor_tensor(out=buf[:, :], in0=otmp[:, :], in1=zT[:, t0:t0 + 512], op=ALU.mult),
                                     waits=[b_, stdone])
                        store_o(None, row0, tok0 + t0, wr)

                LA = 2
                for i in range(min(LA, NS)):
                    issue_W(i)
                for i in range(NS):
                    issue_exp(i)
                    issue_PV(i)
                    if i + LA < NS:
                        issue_W(i + LA)
                return [pv_ev[NS - 1], e_ev[NS - 1], (SE["dve"], SE["dve"].n)]

            def sbatt(qT, kT, vT, zT, V, Pt, otmp, SP, Et, CSs, row0, tok0):
                vdone = v_transpose(vT, V)
                R.wait("pe", *vdone)
                Wb = [banks[0], banks[1], banks[2]]
                OUTb = [banks[3], banks[4]]
                CSb = [banks[5], banks[6]]
                st = sbatt.__dict__
                wi = [0]
                ex_ev = {}
                epi_ev = {}
                cs_dve = {}
                w2_ev_prev = {}
                pvlast = None
                def do_qtile(qt):
                    n = 4 * qt + 4
                    qs = qT[:, qt * 512:(qt + 1) * 512]
                    z_ev = [None] * n
                    e1_ev = [None] * n
                    sp_ev = [None] * n
                    z_use = [None] * n

                    def issue_Z(kc):
                        u = wi[0]
                        wi[0] += 1
                        z_use[kc] = u
                        bk_ = Wb[u % 3]
                        w = [ex_ev[u - 3]] if (u - 3) in ex_ev else []
                        diag = kc >= 4 * qt
                        fn = (lambda e: e.matmul(bk_[:, :], lhsT=kT[:, kc * 128:(kc + 1) * 128], rhs=qs, start=True, stop=not diag))
                        if diag:
                            N("pe", fn, waits=w)
                            z_ev[kc] = E("pe", lambda e: e.matmul(bk_[:, :], lhsT=ident[:, :], rhs=cm[:, 4 + kc - 4 * qt, :], start=False, stop=True))
                        else:
                            z_ev[kc] = E("pe", fn, waits=w)

                    def issue_exp1(kc):
                        u = z_use[kc]
                        bk_ = Wb[u % 3]
                        ex_ev[u] = E("act", lambda e: e.activation(out=Et[kc % 2][:, :], in_=bk_[:, :], func=AF.Exp), waits=[z_ev[kc]])
                        e1_ev[kc] = ex_ev[u]

                    def issue_ln(kc):
                        w = [e1_ev[kc]] + ([w2_ev_prev[kc]] if kc in w2_ev_prev else [])
                        sp_ev[kc] = E("act", lambda e: e.activation(out=SP[:, kc, :], in_=Et[kc % 2][:, :], func=AF.Ln, bias=1.0, scale=1.0), waits=w)

                    def issue_CS(kc):
                        w = [sp_ev[kc]]
                        if kc == 0 and (qt - 2) in cs_dve:
                            w.append(cs_dve[qt - 2])
                        fn = (lambda e: e.matmul(CSb[qt % 2][0:64, :], lhsT=z95[:, 31 - kc:95 - kc], rhs=SP[:, kc, :], start=(kc == 0), stop=(kc == n - 1)))
                        if kc == n - 1:
                            return E("pe", fn, waits=w)
                        N("pe", fn, waits=w)
                        return None

                    issue_Z(0)
                    if n > 1:
                        issue_Z(1)
                    cs_pe = None
                    for kc in range(n):
                        issue_exp1(kc)
                        if kc >= 1:
                            issue_ln(kc - 1)
                            r = issue_CS(kc - 1)
                        if kc + 2 < n:
                            issue_Z(kc + 2)
                    issue_ln(n - 1)
                    cs_pe = issue_CS(n - 1)
                    cs = CSs[qt % 2]
                    w = [cs_pe] + ([epi_ev[qt - 2]] if (qt - 2) in epi_ev else [])
                    a = E("dve", lambda e, cs=cs: e.tensor_copy(out=cs[0:64, :], in_=CSb[qt % 2][0:64, :]), waits=w)
                    cs_dve[qt] = E("dve", lambda e, cs=cs: e.tensor_tensor(out=cs[32:64, :], in0=CSb[qt % 2][32:64, :], in1=cs[32:64, :],
                                                                          op=ALU.subtract), waits=[a])
                    w2_ev = [None] * n
                    a_ev = [None] * n
                    pv_ev = [None] * n
                    w2_use = [None] * n

                    def issue_W2(kc):
                        u = wi[0]
                        wi[0] += 1
                        w2_use[kc] = u
                        bk_ = Wb[u % 3]
                        w = [cs_dve[qt]] + ([ex_ev[u - 3]] if (u - 3) in ex_ev else [])
                        diag = kc >= 4 * qt
                        N("pe", lambda e: e.matmul(bk_[:, :], lhsT=kT[:, kc * 128:(kc + 1) * 128], rhs=qs, start=True, stop=False), waits=w)
                        N("pe", lambda e: e.matmul(bk_[:, :], lhsT=tincl[:, :], rhs=SP[:, kc, :], start=False, stop=False))
                        fn = (lambda e: e.matmul(bk_[:, :], lhsT=u2[:, kc, :], rhs=cs[0:64, :], start=False, stop=not diag))
                        if diag:
                            N("pe", fn)
                            w2_ev[kc] = E("pe", lambda e: e.matmul(bk_[:, :], lhsT=ident[:, :], rhs=cm[:, 4 + kc - 4 * qt, :], start=False, stop=True))
                        else:
                            w2_ev[kc] = E("pe", fn)
                        w2_ev_prev[kc] = w2_ev[kc]

                    pcnt = st.setdefault("pcnt", 0)

                    def issue_exp2(kc):
                        u = w2_use[kc]
                        bk_ = Wb[u % 3]
                        pi = st["pcnt"]
                        st["pcnt"] += 1
                        pbuf = Pt[pi % 3]
                        w = [w2_ev[kc]]
                        pvq = st.setdefault("pvq", {})
                        if (pi - 3) in pvq:
                            w.append(pvq[pi - 3])
                        ex_ev[u] = E("act", lambda e: e.activation(out=pbuf[:, :], in_=bk_[:, :], func=AF.Exp), waits=w)
                        a_ev[kc] = (ex_ev[u], pbuf, pi)

                    def issue_PV(kc):
                        ev, pbuf, pi = a_ev[kc]
                        w = [ev]
                        if kc == 0 and (qt - 2) in epi_ev:
                            w.append(epi_ev[qt - 2])
                        pv_ev[kc] = E("pe", lambda e: e.matmul(OUTb[qt % 2][:, :], lhsT=V[:, kc, :], rhs=pbuf[:, :], start=(kc == 0), stop=(kc == n - 1)),
                                      waits=w)
                        st["pvq"][pi] = pv_ev[kc]

                    issue_W2(0)
                    if n > 1:
                        issue_W2(1)
                    for kc in range(n):
                        issue_exp2(kc)
                        issue_PV(kc)
                        if kc + 2 < n:
                            issue_W2(kc + 2)
                    pvlast = pv_ev[n - 1]
                    t0 = qt * 512

                    def wr(buf, stdone, t0=t0, qt=qt, pvl=pvlast):
                        ev = E("dve", lambda e: e.tensor_tensor(out=buf[:, :], in0=OUTb[qt % 2][:, :], in1=zT[:, t0:t0 + 512], op=ALU.mult),
                               waits=[pvl, stdone])
                        epi_ev[qt] = ev
                        return ev
                    store_o(None, row0, tok0 + t0, wr)
                    return pvlast

                for qt in range(TPS):
                    pvlast = do_qtile(qt)
                last_ex = ex_ev[wi[0] - 1]
                return [pvlast, last_ex, (SE["dve"], SE["dve"].n), (SE["act"], SE["act"].n)]

            def conv_units():
                with ExitStack() as ps:
                    xc = [SBs(ps, f"xc{i}", [128, 514], F32) for i in range(2)]
                    ctmp = SBs(ps, "ctmp", [128, 512], F32)
                    ytmp = SBs(ps, "ytmp", [128, 512], F32)
                    stmp = SBs(ps, "stmp", [128, 512], F32)
                    cprev = {"mult": None, "fin": None}
                    for j in range(2):
                        w_ready = load_weights(2048 + j * 512, 4)

                        def evac_tile(ti, tt, bs, stops, j=j):
                            cur = xc[ti % 2]
                            nxt = xc[(ti + 1) % 2]
                            if tt % TPS == 0:
                                N("dve", lambda e: e.memset(cur[:, 0:2], 0.0))
                            a = E("act", lambda e: e.activation(out=ctmp[:, :], in_=bs[1][:, :], func=AF.Copy),
                                  waits=[stops[1]] + ([cprev["mult"]] if cprev["mult"] else []))
                            cprev["mult"] = E("dve", lambda e: e.tensor_tensor(out=cur[:, 2:514], in0=ctmp[:, :], in1=bs[2][:, :], op=ALU.mult),
                                              waits=[a, stops[2]])
                            N("dve", lambda e: e.tensor_scalar(out=ytmp[:, :], in0=cur[:, 0:512], scalar1=cw[:, j * 3:j * 3 + 1], scalar2=None, op0=ALU.mult))
                            N("dve", lambda e: e.scalar_tensor_tensor(out=ytmp[:, :], in0=cur[:, 1:513], scalar=cw[:, j * 3 + 1:j * 3 + 2], in1=ytmp[:, :],
                                                                      op0=ALU.mult, op1=ALU.add))
                            N("dve", lambda e: e.scalar_tensor_tensor(out=ytmp[:, :], in0=cur[:, 2:514], scalar=cw[:, j * 3 + 2:j * 3 + 3], in1=ytmp[:, :],
                                                                      op0=ALU.mult, op1=ALU.add))
                            N("dve", lambda e: e.tensor_copy(out=nxt[:, 0:2], in_=cur[:, 512:514]))
                            s_ = E("act", lambda e: e.activation(out=stmp[:, :], in_=bs[3][:, :], func=AF.Silu),
                                   waits=[stops[3]] + ([cprev["fin"]] if cprev["fin"] else []))
                            b_ = E("dve", lambda e: e.tensor_tensor(out=ytmp[:, :], in0=ytmp[:, :], in1=bs[0][:, :], op=ALU.mult), waits=[stops[0]])
                            res = {}

                            def wr(buf, stdone):
                                res["ev"] = E("dve", lambda e: e.tensor_tensor(out=buf[:, :], in0=ytmp[:, :], in1=stmp[:, :], op=ALU.mult),
                                              waits=[s_, b_, stdone])
                                return res["ev"]
                            store_o(None, 256 + j * 128, tt * 512, wr)
                            cprev["fin"] = res["ev"]
                            return [res["ev"], s_]

                        project(4, list(range(NTT)), evac_tile, w_ready, [])
                    for ev in all_tile_events():
                        for eng in ("pe", "act", "dve", "pool", "sp"):
                            R.wait(eng, *ev)
                    section_end()

            def gate_units():
                with ExitStack() as ps:
                    blocks = [(i, j) for i in range(3) for j in range(KCO)]
                    for u0 in range(0, len(blocks), 4):
                        ub = blocks[u0:u0 + 4]
                        w_ready = load_weights(3072 + u0 * 128, len(ub))

                        def evac_tile(ti, tt, bs, stops, ub=ub):
                            evs = []
                            for ci, (i, j) in enumerate(ub):
                                idx = i * KCO + j

                                def wr(buf, stdone, ci=ci, idx=idx):
                                    ev = E("act", lambda e: e.activation(out=buf[:, :], in_=bs[ci][:, :], func=AF.Sigmoid,
                                                                         bias=bmg[:, idx:idx + 1], scale=1.0), waits=[stops[ci], stdone])
                                    evs.append(ev)
                                    return ev
                                k = state["ocnt"]
                                state["ocnt"] += 1
                                b = k % ost.n
                                ev = wr(ost.t[b], ost.st_done(k))
                                ost.store(k, sig[idx * 128:(idx + 1) * 128, tt * 512:(tt + 1) * 512], ost.t[b][:], waits=[ev])
                            return evs

                        project(len(ub), list(range(NTT)), evac_tile, w_ready, [])
                    for ev in all_tile_events():
                        for eng in ("pe", "act", "dve", "pool", "sp"):
                            R.wait(eng, *ev)
                    section_end()

            which = cfg.__dict__.get("units", "ACBG")
            if "A" in which:
                attention_units("A")
            if "C" in which:
                attention_units("C")
            if "B" in which:
                conv_units()
            if "G" in which:
                gate_units()
            for i in range(ost.n):
                R.wait("pool", *ost.st_done(i))
            section_end()
            common.close()

        def phase3():
            with ExitStack() as ps:
                pw = SBs(ps, "pw", [128, 3, 16, DC], BF16)
                pst = Ring(ps, "pst", 2, [128, 4, DC], F32)
                of = [Ring(ps, f"of{i}", 2, [128, 8, 2, 512], BF16) for i in range(3)]
                sg = Ring(ps, "sg", 2, [128, 3 * KCO, 512], BF16)
                ya = SBs(ps, "ya", [128, 512], F32)
                yb = SBs(ps, "yb", [128, 512], F32)
                yst = Ring(ps, "yst", 3, [128, 512], BF16)
                cnt = 0
                cast_prev = {}
                cast_ev = None
                for i in range(3):
                    for g in range(4):
                        b = cnt % 2
                        ld = pst.load(cnt, pst.t[b][:], p_w[i, g * 512:(g + 1) * 512, :].rearrange("(k p) c -> p k c", p=128),
                                      waits=[cast_prev[cnt - 2]] if (cnt - 2) in cast_prev else [])
                        cast_ev = E("pool", lambda e, b=b, i=i, g=g: e.tensor_copy(out=pw[:, i, g * 4:(g + 1) * 4, :], in_=pst.t[b][:]),
                                    waits=[ld])
                        cast_prev[cnt] = cast_ev
                        cnt += 1
                R.wait("pe", *cast_ev)
                ofv = ofull.rearrange("(r i j p) t -> i p r j t", r=8, i=3, j=2, p=128)
                sgv = sig.rearrange("(x p) t -> p x t", p=128)
                slot = 0
                dve_done = {}
                last_pe_tile = {}
                ycnt = 0
                for tt in range(NTT):
                    b = tt % 2
                    pwait = [last_pe_tile[tt - 2]] if (tt - 2) in last_pe_tile else []
                    lds = []
                    for i in range(3):
                        for j2 in range(2):
                            ld_ = of[i].load(tt, of[i].t[b][:, :, j2, :], ofv[i][:, :, j2, tt * 512:(tt + 1) * 512], waits=pwait)
                        lds.append(ld_)
                    lsg = sg.load(tt, sg.t[b][:], sgv[:, :, tt * 512:(tt + 1) * 512],
                                  waits=[dve_done[slot - 1 - KCO]] if (slot - 1 - KCO) in dve_done else [])
                    for ld in lds:
                        R.wait("pe", *ld)
                    for j in range(KCO):
                        bs = [banks[(slot % 2) * 3 + i] for i in range(3)]
                        if (slot - 2) in dve_done:
                            R.wait("pe", *dve_done[slot - 2])
                        stops = []
                        for i in range(3):
                            for k in range(16):
                                fn = (lambda e, i=i, k=k, b=b, j=j, bs=bs: e.matmul(
                                    bs[i][:, :], lhsT=pw[:, i, k, j * 128:(j + 1) * 128], rhs=of[i].t[b][:, k // 2, k % 2, :],
                                    start=(k == 0), stop=(k == 15)))
                                if k == 15:
                                    stops.append(E("pe", fn))
                                else:
                                    N("pe", fn)
                        last_pe_tile[tt] = stops[-1]
                        E("dve", lambda e, b=b, j=j, bs=bs: e.tensor_tensor(out=ya[:, :], in0=bs[0][:, :], in1=sg.t[b][:, 0 * KCO + j, :], op=ALU.mult),
                          waits=[stops[0], lsg])
                        E("dve", lambda e, b=b, j=j, bs=bs: e.tensor_tensor(out=yb[:, :], in0=bs[1][:, :], in1=sg.t[b][:, 1 * KCO + j, :], op=ALU.mult),
                          waits=[stops[1]])
                        E("dve", lambda e: e.tensor_tensor(out=ya[:, :], in0=ya[:, :], in1=yb[:, :], op=ALU.add))
                        dve_done[slot] = E("dve", lambda e, b=b, j=j, bs=bs: e.tensor_tensor(out=yb[:, :], in0=bs[2][:, :], in1=sg.t[b][:, 2 * KCO + j, :],
                                                                                          op=ALU.mult), waits=[stops[2]])
                        yb_ = ycnt % yst.n
                        ev = E("dve", lambda e, yb_=yb_: e.tensor_tensor(out=yst.t[yb_][:, :], in0=ya[:, :], in1=yb[:, :], op=ALU.add),
                               waits=[yst.st_done(ycnt)])
                        yst.store(ycnt, yin[j * 128:(j + 1) * 128, tt * 512:(tt + 1) * 512], yst.t[yb_][:], waits=[ev])
                        ycnt += 1
                        slot += 1
                for i in range(yst.n):
                    R.wait("pool", *yst.st_done(i))
                section_end()

        def phase4():
            common = ExitStack()
            PJ = ProjEngine(common)
            ost = Ring(common, "o4", 4, [128, 512], F32)
            sq = [SBs(common, f"q4{i}", [128, KCO, 512], BF16) for i in range(2)]
            sst = Ring(common, "t4", 2, [1, 512], F32)
            w_ready = PJ.load_weights(w_o, 0, KCO)
            cnts = {"o": 0}
            cp_prev = {}

            def evac_tile(ti, tt, bs, stops):
                evs = []
                b = ti % 2
                for j in range(KCO):
                    k = cnts["o"]
                    cnts["o"] += 1
                    ob = k % ost.n
                    ev = E("act", lambda e, ob=ob, j=j: e.activation(out=ost.t[ob][:, :], in_=bs[j][:, :], func=AF.Copy),
                           waits=[stops[j], ost.st_done(k)])
                    ost.store(k, oscr[j * 128:(j + 1) * 128, tt * 512:(tt + 1) * 512], ost.t[ob][:], waits=[ev])
                    w = [cp_prev[ti - 2]] if (j == 0 and (ti - 2) in cp_prev) else []
                    evs.append(E("act", lambda e, b=b, j=j: e.activation(out=sq[b][:, j, :], in_=bs[j][:, :], func=AF.Square), waits=w))
                cp = sumsq_tile(tt, [sq[b][:, j, :] for j in range(KCO)], sst, ti, [evs[-1]] + ([cp_prev[ti - 1]] if (ti - 1) in cp_prev else []),
                                bs[0])
                cp_prev[ti] = cp
                return evs + [cp]

            PJ.project(yfull, KCO, list(range(NTT)), evac_tile, w_ready, [])
            for i in range(ost.n):
                R.wait("pool", *ost.st_done(i))
            for i in range(2):
                R.wait("pool", *sst.st_done(i))
            section_end()
            common.close()

        if seg == "S0":
            phase0(xT)
        elif seg == "S1":
            phase1(xT)
        elif seg == "S2":
            phase2()
        elif seg == "S3":
            phase3()
        elif seg == "S4":
            phase4()
        elif seg == "S5":
            phase5(xT)
        R.flush(nc)
    return nc


def _core_cols(cfg, c):
    D, DC, KCO = cfg.D, cfg.DC, cfg.KCO
    cols = []
    for base in (0, 8 * W):
        for hh in range(2):
            h = 2 * c + hh
            for part in range(4):
                o = base + part * W + h * 128
                cols.append(np.arange(o, o + 128))
    for j in range(2):
        ch = c * 256 + j * 128
        for part in range(4):
            o = 4 * W + part * W + ch
            cols.append(np.arange(o, o + 128))
    for i in range(3):
        for j in range(KCO):
            o = 12 * W + i * D + c * DC + j * 128
            cols.append(np.arange(o, o + 128))
    return np.concatenate(cols)


_NC_CACHE = {}


def _run(cfg, seg, in_maps):
    key = (cfg.D, cfg.S, cfg.NB, seg, getattr(cfg, "units", "ACBG"))
    if key not in _NC_CACHE:
        _NC_CACHE[key] = build(cfg, seg)
    res = run_bass_kernel_spmd(_NC_CACHE[key], in_maps, core_ids=list(range(NCORES)))
    return res.results


def forward(cfg, x, pre_norm_gain, post_norm_gain, w_in, b_merge_gate, conv_w, w_branch_a, w_branch_b,
            w_branch_c, w_out, trace=None):
    D, DC, KCO, DEPTH, NT = cfg.D, cfg.DC, cfg.KCO, cfg.DEPTH, cfg.NT
    C = range(NCORES)
    consts = host_consts(cfg)
    f32 = np.float32
    xT_full = np.ascontiguousarray(np.asarray(x, f32).reshape(NT, D).T)
    xs = [np.ascontiguousarray(xT_full[c * DC:(c + 1) * DC]) for c in C]
    del xT_full
    r = _run(cfg, "S0", [{"xT": xs[c]} for c in C])
    ssall = np.ascontiguousarray(np.stack([r[c]["ssp"] for c in C], 0))
    w_in = np.asarray(w_in, f32)
    for l in range(DEPTH):
        def gslice(g, c):
            return np.ascontiguousarray(np.asarray(g, f32)[l, c * DC:(c + 1) * DC].reshape(KCO, 128).T)
        r = _run(cfg, "S1", [{"xT": xs[c], "ssall": ssall, "pre_g": gslice(pre_norm_gain, c)} for c in C])
        hfull = np.ascontiguousarray(np.concatenate([r[c]["hin"] for c in C], 0))
        if trace is not None:
            trace[f"hfull{l}"] = hfull
        maps = []
        for c in C:
            m = {"hfull": hfull}
            m["w_in"] = np.ascontiguousarray(w_in[l][:, _core_cols(cfg, c)])[None]
            bm = np.asarray(b_merge_gate, f32)[l].reshape(3, D)[:, c * DC:(c + 1) * DC]
            m["b_mg"] = np.ascontiguousarray(bm.reshape(3, KCO, 128).transpose(2, 0, 1).reshape(128, 3 * KCO))
            cwc = np.asarray(conv_w, f32)[l][:, c * 256:(c + 1) * 256]
            m["conv_w"] = np.ascontiguousarray(cwc.reshape(3, 2, 128).transpose(2, 1, 0).reshape(128, 6))
            m.update(consts)
            maps.append(m)
        r = _run(cfg, "S2", maps)
        del maps
        ofull = np.ascontiguousarray(np.concatenate([r[c]["oin"] for c in C], 0))
        sigs = [r[c]["sig"] for c in C]
        if trace is not None:
            trace[f"ofull{l}"] = ofull
            trace[f"sig{l}"] = sigs
        maps = []
        for c in C:
            pw = np.stack([np.asarray(wb, f32)[l][:, c * DC:(c + 1) * DC] for wb in (w_branch_a, w_branch_b, w_branch_c)], 0)
            maps.append({"ofull": ofull, "sig": sigs[c], "p_w": np.ascontiguousarray(pw)})
        r = _run(cfg, "S3", maps)
        del maps
        yfull = np.ascontiguousarray(np.concatenate([r[c]["yin"] for c in C], 0))
        if trace is not None:
            trace[f"yfull{l}"] = yfull
        r = _run(cfg, "S4", [{"yfull": yfull, "w_o": np.ascontiguousarray(np.asarray(w_out, f32)[l][:, c * DC:(c + 1) * DC])} for c in C])
        oscr = [r[c]["oscr"] for c in C]
        ss2 = np.ascontiguousarray(np.stack([r[c]["ssp"] for c in C], 0))
        if trace is not None:
            trace[f"oscr{l}"] = np.concatenate(oscr, 0)
        r = _run(cfg, "S5", [{"oscr": oscr[c], "xT": xs[c], "ssall": ss2, "post_g": gslice(post_norm_gain, c)} for c in C])
        xs = [r[c]["xnT"] for c in C]
        ssall = np.ascontiguousarray(np.stack([r[c]["ssp"] for c in C], 0))
    outT = np.concatenate(xs, 0)
    return np.ascontiguousarray(outT.T).reshape(cfg.NB, cfg.S, cfg.D).astype(np.float32)


def kernel(x, pre_norm_gain, post_norm_gain, w_in, b_merge_gate, conv_w, w_branch_a, w_branch_b, w_branch_c,
           w_out):
    cfg = Cfg()
    return forward(cfg, x, pre_norm_gain, post_norm_gain, w_in, b_merge_gate, conv_w, w_branch_a, w_branch_b,
                   w_branch_c, w_out)
```
